# Optimizing a Trainium2 kernel written in Bass

```python
import jax, jax.numpy as jnp
from jax import lax
import numpy as np

D_MODEL = 2048
BATCH = 1
SEQ = 8192
DEPTH = 1

D_MIX = D_MODEL
D_LRU = D_MIX // 2
D_RET = D_MIX - D_LRU
LRU_BLOCKS = 8
LRU_BLOCK_W = D_LRU // LRU_BLOCKS
CONV_W = 4
LRU_C = 8.0
RET_HEADS = 8
RET_HD = D_RET // RET_HEADS
CHUNK = 128
ROPE_BASE = 10000.0
D_FF = ((8 * D_MODEL + 3 * 256 - 1) // (3 * 256)) * 256
EPS = 1e-6
SPLITS = (D_LRU, 2 * D_LRU, 2 * D_LRU + D_RET, 2 * D_LRU + 2 * D_RET, 2 * D_LRU + 3 * D_RET)
D_IN = 2 * D_LRU + 4 * D_RET

kernel_name = "hymba_rglru_retention_hybrid"


def rmsnorm(x, w):
    xf = x.astype(jnp.float32)
    y = xf * lax.rsqrt(jnp.mean(xf * xf, axis=-1, keepdims=True) + EPS)
    return (y * w.astype(jnp.float32)).astype(x.dtype)


def causal_depthwise_conv(x, w, b):
    S = x.shape[1]
    xp = jnp.pad(x, ((0, 0), (CONV_W - 1, 0), (0, 0)))
    y = b
    for tap in range(CONV_W):
        y = y + xp[:, tap:tap + S, :] * w[tap]
    return y


def rg_lru(u, wa, ba, wx, bx, lam):
    B, S, _ = u.shape
    uf = u.astype(jnp.float32)
    ub = uf.reshape(B, S, LRU_BLOCKS, LRU_BLOCK_W)
    r = jax.nn.sigmoid(jnp.einsum('bsnc,ncd->bsnd', ub, wa.astype(jnp.float32)).reshape(B, S, D_LRU) + ba)
    i = jax.nn.sigmoid(jnp.einsum('bsnc,ncd->bsnd', ub, wx.astype(jnp.float32)).reshape(B, S, D_LRU) + bx)
    log_a = LRU_C * r * jax.nn.log_sigmoid(lam.astype(jnp.float32))
    a = jnp.exp(log_a)
    b = jnp.sqrt(-jnp.expm1(2.0 * log_a)) * (i * uf)

    def combine(left, right):
        a1, b1 = left
        a2, b2 = right
        return a1 * a2, a2 * b1 + b2

    _, h = lax.associative_scan(combine, (a, b), axis=1)
    return h.astype(u.dtype)


def rope(t, cos, sin):
    half = t.shape[-1] // 2
    t1, t2 = t[..., :half], t[..., half:]
    c = cos[None, :, None, :]
    s = sin[None, :, None, :]
    return jnp.concatenate([t1 * c - t2 * s, t1 * s + t2 * c], axis=-1)


def retention(q, k, v, g, gn_w):
    B, S, _ = q.shape
    H, Dh, C = RET_HEADS, RET_HD, CHUNK
    N = S // C
    f32 = jnp.float32
    pos = jnp.arange(S, dtype=f32)
    inv_freq = ROPE_BASE ** (-jnp.arange(0, Dh, 2, dtype=f32) / Dh)
    ang = pos[:, None] * inv_freq[None, :]
    cos, sin = jnp.cos(ang), jnp.sin(ang)
    qh = rope(q.astype(f32).reshape(B, S, H, Dh), cos, sin)
    kh = rope(k.astype(f32).reshape(B, S, H, Dh), cos, sin) * (Dh ** -0.5)
    vh = v.astype(f32).reshape(B, S, H, Dh)
    qc = qh.reshape(B, N, C, H, Dh)
    kc = kh.reshape(B, N, C, H, Dh)
    vc = vh.reshape(B, N, C, H, Dh)

    log_gamma = jnp.log1p(-jnp.exp2(-5.0 - jnp.arange(H, dtype=f32)))
    idx = jnp.arange(C)
    diff = idx[:, None] - idx[None, :]
    causal = diff >= 0
    decay = jnp.where(causal[None], jnp.exp(log_gamma[:, None, None] * jnp.where(causal, diff, 0)[None].astype(f32)), 0.0)

    scores = jnp.einsum('bnqhd,bnkhd->bnhqk', qc, kc) * decay[None, None]
    inner = jnp.einsum('bnhqk,bnkhe->bnqhe', scores, vc)

    zeta = jnp.exp(log_gamma[None, :] * (C - 1 - idx).astype(f32)[:, None])
    kv = jnp.einsum('bnkhd,bnkhe->bnhde', kc * zeta[:, :, None], vc)
    chunk_decay = jnp.exp(log_gamma * C)[:, None, None]

    def step(R, kv_n):
        return R * chunk_decay + kv_n, R

    _, R_prev = lax.scan(step, jnp.zeros((B, H, Dh, Dh), f32), jnp.moveaxis(kv, 1, 0))
    R_prev = jnp.moveaxis(R_prev, 0, 1)

    xi = jnp.exp(log_gamma[None, :] * (idx + 1).astype(f32)[:, None])
    cross = jnp.einsum('bnqhd,bnhde->bnqhe', qc * xi[:, :, None], R_prev)
    o = (inner + cross).reshape(B, S, H, Dh)

    mu = jnp.mean(o, axis=-1, keepdims=True)
    var = jnp.mean(jnp.square(o - mu), axis=-1, keepdims=True)
    on = ((o - mu) * lax.rsqrt(var + EPS)).reshape(B, S, D_RET) * gn_w.astype(f32)
    return (jax.nn.silu(g.astype(f32)) * on).astype(q.dtype)


def setup_inputs(seed: int = 0) -> dict:
    key = jax.random.key(seed)
    ks = jax.random.split(key, 20)
    f32 = jnp.float32

    def nrm(k, shape, scale):
        return jax.random.normal(k, shape, f32) * scale

    u = jax.random.uniform(ks[8], (DEPTH, D_LRU), f32, 0.9, 0.999)
    s = u ** (1.0 / LRU_C)
    lru_lambda = jnp.log(s) - jnp.log1p(-s)
    return {
        "x": nrm(ks[0], (BATCH, SEQ, D_MODEL), 1.0),
        "ln1_w": 1.0 + nrm(ks[1], (DEPTH, D_MODEL), 0.02),
        "w_in": nrm(ks[2], (DEPTH, D_MODEL, D_IN), D_MODEL ** -0.5),
        "conv_w": nrm(ks[3], (DEPTH, CONV_W, D_LRU), CONV_W ** -0.5),
        "conv_b": nrm(ks[4], (DEPTH, D_LRU), 0.01),
        "gate_a_w": nrm(ks[5], (DEPTH, LRU_BLOCKS, LRU_BLOCK_W, LRU_BLOCK_W), LRU_BLOCK_W ** -0.5),
        "gate_a_b": nrm(ks[6], (DEPTH, D_LRU), 0.01),
        "gate_x_w": nrm(ks[7], (DEPTH, LRU_BLOCKS, LRU_BLOCK_W, LRU_BLOCK_W), LRU_BLOCK_W ** -0.5),
        "gate_x_b": nrm(ks[9], (DEPTH, D_LRU), 0.01),
        "lru_lambda": lru_lambda,
        "ret_gn_w": 1.0 + nrm(ks[10], (DEPTH, D_RET), 0.02),
        "w_out": nrm(ks[11], (DEPTH, D_MIX, D_MODEL), D_MIX ** -0.5),
        "ln2_w": 1.0 + nrm(ks[12], (DEPTH, D_MODEL), 0.02),
        "w_ffn_gate": nrm(ks[13], (DEPTH, D_MODEL, D_FF), D_MODEL ** -0.5),
        "w_ffn_up": nrm(ks[14], (DEPTH, D_MODEL, D_FF), D_MODEL ** -0.5),
        "w_ffn_down": nrm(ks[15], (DEPTH, D_FF, D_MODEL), D_FF ** -0.5),
        "final_norm_w": 1.0 + nrm(ks[16], (D_MODEL,), 0.02),
    }


def reference(x, ln1_w, w_in, conv_w, conv_b, gate_a_w, gate_a_b, gate_x_w, gate_x_b,
              lru_lambda, ret_gn_w, w_out, ln2_w, w_ffn_gate, w_ffn_up, w_ffn_down, final_norm_w):
    h = x
    for l in range(DEPTH):
        u = rmsnorm(h, ln1_w[l])
        proj = jnp.einsum('bsd,de->bse', u, w_in[l])
        lru_x, lru_g, q, k, v, ret_g = jnp.split(proj, SPLITS, axis=-1)
        lru_x = causal_depthwise_conv(lru_x, conv_w[l], conv_b[l])
        y_lru = rg_lru(lru_x, gate_a_w[l], gate_a_b[l], gate_x_w[l], gate_x_b[l], lru_lambda[l])
        y_lru = y_lru * jax.nn.gelu(lru_g)
        y_ret = retention(q, k, v, ret_g, ret_gn_w[l])
        y = jnp.concatenate([y_lru, y_ret], axis=-1)
        h = h + jnp.einsum('bse,ed->bsd', y, w_out[l])
        u = rmsnorm(h, ln2_w[l])
        ff = jax.nn.silu(jnp.einsum('bsd,df->bsf', u, w_ffn_gate[l])) * jnp.einsum('bsd,df->bsf', u, w_ffn_up[l])
        h = h + jnp.einsum('bsf,fd->bsd', ff, w_ffn_down[l])
    return rmsnorm(h, final_norm_w)
```

```python
import contextlib
import numpy as np
import ml_dtypes
import concourse.bass as bass
import concourse.mybir as mybir
from concourse.bass_utils import run_bass_kernel_spmd

F32 = mybir.dt.float32
BF16 = mybir.dt.bfloat16
AF = mybir.ActivationFunctionType
ALU = mybir.AluOpType

NCORES = 8
T = 1024
DM = 2048
DFF = 5632
NF = 44
EPS = 1e-6
GELU_TANH_NATIVE = False

ENGS = ("pe", "act", "dve", "pool", "sp")


class Buf:
    def __init__(self, name):
        self.name = name
        self.last_write = None
        self.readers = {}
        self.dsem = None
        self.dcount = 0

    def deps_all(self):
        d = dict(self.readers)
        if self.last_write is not None:
            s, v = self.last_write
            if d.get(id(s), (s, 0))[1] < v:
                d[id(s)] = (s, v)
        return d

    def inherit(self, olds):
        for o in olds:
            for k, (s, v) in o.deps_all().items():
                if self.readers.get(k, (s, 0))[1] < v:
                    self.readers[k] = (s, v)


class Kern:
    def __init__(self, nc, stack):
        self.nc = nc
        self.stack = stack
        self.prog = {e: [] for e in ENGS}
        self.cnt = {e: 0 for e in ENGS}
        self.sem = {e: stack.enter_context(nc.semaphore("s_" + e)) for e in ENGS if e != "sp"}
        self.waited = {e: {} for e in ENGS}
        self.nsem = 0
        self.store_deps = {}

    def new_sem(self, name):
        self.nsem += 1
        return self.stack.enter_context(self.nc.semaphore("d%d_%s" % (self.nsem, name)))

    def _collect(self, eng, reads, writes, skip_sem=None):
        need = {}

        def add(dep):
            s, v = dep
            if skip_sem is not None and s is skip_sem:
                return
            if eng == "pe" and s is self.sem["pe"]:
                return
            if need.get(id(s), (s, 0))[1] < v:
                need[id(s)] = (s, v)

        for b in reads:
            if b.last_write is not None:
                add(b.last_write)
        for b in writes:
            if b.last_write is not None:
                add(b.last_write)
            for dep in b.readers.values():
                add(dep)
        waits = []
        for k, (s, v) in need.items():
            if self.waited[eng].get(k, 0) >= v:
                continue
            self.waited[eng][k] = v
            waits.append((s, v))
        return waits

    def op(self, eng, fn, reads=(), writes=(), signal=True):
        waits = self._collect(eng, reads, writes)
        s = self.sem[eng]
        if signal:
            self.cnt[eng] += 1
            me = (s, self.cnt[eng])
            inc = (s, 1)
        else:
            me = (s, self.cnt[eng] + 1)
            inc = None
        self.prog[eng].append((waits, fn, inc))
        for b in reads:
            if b.readers.get(id(s), (s, 0))[1] < me[1]:
                b.readers[id(s)] = me
        for b in writes:
            b.last_write = me
            b.readers = {}

    def dma(self, q, out, in_, reads=(), writes=(), **kw):
        b = writes[0] if writes else reads[0]
        if b.dsem is None:
            b.dsem = self.new_sem(b.name)
        waits = self._collect(q, reads, writes, skip_sem=b.dsem)
        b.dcount += 16
        me = (b.dsem, b.dcount)
        self.prog[q].append((waits, lambda e, o=out, i=in_, k=kw: e.dma_start(out=o, in_=i, **k), (b.dsem, 16)))
        for w in writes:
            w.last_write = me
            w.readers = {}
        for r in reads:
            r.readers[id(b.dsem)] = me
        if not writes:
            self.store_deps[id(b.dsem)] = me

    def finish(self):
        waits = [(s, v) for (s, v) in self.store_deps.values()]
        self.prog["sp"].append((waits, None, None))

    def emit(self, block):
        def run(e, prog):
            for waits, fn, inc in prog:
                for s, v in waits:
                    e.wait_ge(s, v)
                if fn is None:
                    continue
                ins = fn(e)
                if inc is not None:
                    ins.then_inc(inc[0], inc[1])

        @block.tensor
        def _(e):
            run(e, self.prog["pe"])

        @block.scalar
        def _(e):
            run(e, self.prog["act"])

        @block.vector
        def _(e):
            run(e, self.prog["dve"])

        @block.gpsimd
        def _(e):
            run(e, self.prog["pool"])

        @block.sync
        def _(e):
            run(e, self.prog["sp"])


class Region:
    def __init__(self, tens, words):
        self.t = tens
        self.words = words
        self.off = 0
        self.bufs = []
        self.old = []

    def reset(self):
        self.old = self.old + self.bufs
        self.bufs = []
        self.off = 0

    def alloc(self, name, free_elems, dtype):
        nbytes = free_elems * (2 if dtype == BF16 else 4)
        nwords = (nbytes + 3) // 4
        assert self.off + nwords <= self.words, (name, self.off, nwords, self.words)
        v = self.t[:, self.off:self.off + nwords]
        self.off += nwords
        if dtype == BF16:
            v = v.bitcast(BF16)[:, 0:free_elems]
        b = Buf(name)
        b.inherit(self.old)
        self.bufs.append(b)
        return v, b


def build_nc(mode, stop=0):
    nc = bass.Bass("TRN2", target_bir_lowering=False)
    dt = lambda name, shape, dtype=F32, kind="ExternalInput": nc.dram_tensor(name, shape, dtype, kind=kind).ap()
    x_d = dt("x", [T, DM])
    xh_d = dt("xh", [8, DM])
    win_d = dt("win", [48, 128, 2048])
    war_d = dt("war", [128, 1024])
    wxr_d = dt("wxr", [128, 1024])
    lnw_d = dt("lnw", [128, 32])
    smallp_d = dt("smallp", [128, 80])
    ident_d = dt("ident", [128, 128], BF16)
    perm_d = dt("perm", [128, 128], BF16)
    tab_d = dt("tab", [8, 4, 128, 1024])
    if mode == "A":
        st_d = dt("st", [128, 1040], F32, "ExternalOutput")
    else:
        wo_d = dt("wo", [4, 128, 8192])
        wg_d = dt("wg", [NF, 128, 2048])
        wu_d = dt("wu", [NF, 128, 2048])
        wd_d = dt("wd", [8, 128, NF * 256])
        fnw_d = dt("fnw", [128, DM])
        cmask_d = dt("cmask", [128, 1024], BF16)
        onesf_d = dt("onesf", [128, 128])
        rall_d = dt("rall", [8, 128, 1024])
        lall_d = dt("lall", [128, 128])
        coef_d = dt("coef", [128, 72])
        y_d = dt("y", [T, DM], F32, "ExternalOutput")

    with contextlib.ExitStack() as stack:
        def sb(name, shape, dtype=F32):
            return stack.enter_context(nc.sbuf_tensor("sb_" + name, shape, dtype))

        k = Kern(nc, stack)
        ident = sb("ident", [128, 128], BF16); ident_b = Buf("ident")
        perm = sb("perm", [128, 128], BF16); perm_b = Buf("perm")
        lnw = sb("lnw", [128, 32]); lnw_b = Buf("lnw")
        smallp = sb("smallp", [128, 80]); smallp_b = Buf("smallp")
        wa = sb("wa", [128, 1024], BF16); wa_b = Buf("wa")
        wx = sb("wx", [128, 1024], BF16); wx_b = Buf("wx")
        epst = sb("epst", [128, 1]); epst_b = Buf("epst")
        onet = sb("onet", [128, 1]); onet_b = Buf("onet")
        c8 = sb("c8", [128, 16]); c8_b = Buf("c8")
        tmp8 = sb("tmp8", [128, 16]); tmp8_b = Buf("tmp8")
        ss = [sb("ss%d" % i, [128, 2]) for i in range(2)]
        ss_b = [Buf("ss%d" % i) for i in range(2)]
        uT = sb("uT", [128, 16, T], BF16); uT_b = Buf("uT")
        uTh = sb("uTh", [128, 16, 8], BF16); uTh_b = Buf("uTh")
        stl = sb("stl", [128, 16]); stl_b = Buf("stl")
        sumr = sb("sumr", [128, 1]); sumr_b = Buf("sumr")
        r1 = Region(sb("R1", [128, 16384]), 16384)
        r2 = Region(sb("R2", [128, 11264]), 11264)
        if mode == "B":
            yT = sb("yT", [128, 16, T], BF16); yT_b = [Buf("yT%d" % i) for i in range(16)]
            cmask = sb("cmask", [128, 1024], BF16); cmask_b = Buf("cmask")
            onesf = sb("onesf", [128, 128]); onesf_b = Buf("onesf")
            lall = sb("lall", [128, 128]); lall_b = Buf("lall")
            coef = sb("coef", [128, 72]); coef_b = Buf("coef")
            hin = sb("hin", [128, 8]); hin_b = Buf("hin")
            htmp = sb("htmp", [128, 8]); htmp_b = Buf("htmp")
            rin = sb("rin", [128, 1024], BF16); rin_b = Buf("rin")
            wdr = [sb("wdr%d" % i, [128, 22, 256], BF16) for i in range(2)]
            wdr_b = [Buf("wdr%d" % i) for i in range(2)]
        pp = [stack.enter_context(nc.psum_tensor("pp%d" % i, [128, 1024], F32)) for i in range(4)]
        pb = [Buf("bank%d" % i) for i in range(8)]

        def pair_b(i):
            return [pb[2 * i], pb[2 * i + 1]]

        block = stack.enter_context(nc.Block())

        k.dma("sp", ident[:], ident_d[:, :], writes=[ident_b])
        k.dma("sp", perm[:], perm_d[:, :], writes=[perm_b])
        k.dma("sp", lnw[:], lnw_d[:, :], writes=[lnw_b])
        k.dma("sp", smallp[:], smallp_d[:, :], writes=[smallp_b])
        k.dma("pool", wa[:], war_d[:, :], writes=[wa_b])
        k.dma("pool", wx[:], wxr_d[:, :], writes=[wx_b])
        if mode == "B":
            k.dma("sp", cmask[:], cmask_d[:, :], writes=[cmask_b])
            k.dma("sp", onesf[:], onesf_d[:, :], writes=[onesf_b])
            k.dma("sp", lall[:], lall_d[:, :], writes=[lall_b])
            k.dma("sp", coef[:], coef_d[:, :], writes=[coef_b])
        k.op("dve", lambda e: e.memset(epst[:], EPS), writes=[epst_b])
        k.op("dve", lambda e: e.memset(onet[:], 1.0), writes=[onet_b])

        sp3 = smallp[:, 0:72].rearrange("p (j n) -> p j n", n=9)

        def spj(j, n):
            return sp3[:, j, n:n + 1]

        lamv = sp3[:, :, 7]
        k.op("act", lambda e: e.activation(out=tmp8[:, 0:8], in_=lamv, func=AF.Exp, scale=-1.0),
             reads=[smallp_b], writes=[tmp8_b])
        k.op("dve", lambda e: e.tensor_scalar(out=tmp8[:, 8:16], in0=tmp8[:, 0:8], scalar1=-1.0 / 3.0, scalar2=0.5,
                                              op0=ALU.mult, op1=ALU.add), reads=[tmp8_b], writes=[tmp8_b])
        k.op("dve", lambda e: e.tensor_tensor(out=tmp8[:, 8:16], in0=tmp8[:, 8:16], in1=tmp8[:, 0:8], op=ALU.mult),
             reads=[tmp8_b], writes=[tmp8_b])
        k.op("dve", lambda e: e.tensor_scalar(out=tmp8[:, 8:16], in0=tmp8[:, 8:16], scalar1=-1.0, scalar2=1.0,
                                              op0=ALU.mult, op1=ALU.add), reads=[tmp8_b], writes=[tmp8_b])
        k.op("dve", lambda e: e.tensor_tensor(out=tmp8[:, 8:16], in0=tmp8[:, 8:16], in1=tmp8[:, 0:8], op=ALU.mult),
             reads=[tmp8_b], writes=[tmp8_b])
        k.op("dve", lambda e: e.tensor_scalar(out=c8[:, 0:8], in0=tmp8[:, 8:16], scalar1=-8.0, scalar2=None,
                                              op0=ALU.mult), reads=[tmp8_b], writes=[c8_b])
        k.op("dve", lambda e: e.tensor_scalar(out=c8[:, 8:16], in0=tmp8[:, 8:16], scalar1=-16.0, scalar2=None,
                                              op0=ALU.mult), reads=[tmp8_b], writes=[c8_b])

        def early(src_ap, src_b, w):
            k.dma("sp", st_d[:, 0:w], src_ap, reads=[src_b])
            k.finish()
            k.emit(block)

        if stop == 1:
            early(c8[:, :], c8_b, 16)
            return nc
        xt = [r1.alloc("xt%d" % i, DM, F32) for i in range(3)]
        for tt in range(8):
            k.dma("sp", xt[tt % 3][0][:, :], x_d[tt * 128:(tt + 1) * 128, :], writes=[xt[tt % 3][1]]) if tt < 3 else None
        loads_issued = 3
        xh_v, xh_b = r1.alloc("xh", DM, F32)
        k.dma("sp", xh_v[0:8, :], xh_d[:, :], writes=[xh_b])

        def x_src(tt):
            return xt[tt % 3][0][:, :]

        def x_srcb(tt):
            return [xt[tt % 3][1]]

        junk0, junk0_b = r1.alloc("junk", DM, BF16)
        ut0 = [r1.alloc("ut%d" % i, DM, BF16) for i in range(2)]

        def norm_tile(src, srcb, tt, parts, lw_col0, dst_ap, dst_b):
            s_t, s_b = ss[tt % 2], ss_b[tt % 2]
            u_t, u_b = ut0[tt % 2]
            jk, jk_b = junk0, junk0_b
            k.op("act", lambda e: e.activation(out=jk[0:parts, :], in_=src, func=AF.Square,
                                               accum_out=s_t[0:parts, 0:1]),
                 reads=srcb, writes=[jk_b, s_b])
            k.op("act", lambda e: e.activation(out=s_t[0:parts, 1:2], in_=s_t[0:parts, 0:1], func=AF.Sqrt,
                                               bias=epst[0:parts, :], scale=1.0 / DM),
                 reads=[s_b, epst_b], writes=[s_b])
            k.op("dve", lambda e: e.reciprocal(out=s_t[0:parts, 0:1], in_=s_t[0:parts, 1:2]),
                 reads=[s_b], writes=[s_b])
            k.op("act", lambda e: e.activation(out=u_t[0:parts, :], in_=src, func=AF.Copy,
                                               scale=s_t[0:parts, 0:1]),
                 reads=srcb + [s_b], writes=[u_b])
            pi = tt % 2
            pv = pp[pi][:, :].bitcast(BF16)
            for c in range(16):
                k.op("pe", lambda e, c=c: e.transpose(pv[:, c * 128:c * 128 + parts],
                                                      u_t[0:parts, c * 128:(c + 1) * 128],
                                                      ident[0:parts, 0:parts]),
                     reads=[u_b, ident_b], writes=pair_b(pi), signal=(c == 15))
            pv3 = pv.rearrange("p (c t) -> p c t", t=128)[:, :, 0:parts]
            lw = lnw[:, lw_col0:lw_col0 + 16].unsqueeze(2).to_broadcast([128, 16, parts])
            k.op("dve", lambda e: e.tensor_tensor(out=dst_ap, in0=pv3, in1=lw, op=ALU.mult),
                 reads=pair_b(pi) + [lnw_b], writes=[dst_b])

        for tt in range(8):
            norm_tile(x_src(tt), x_srcb(tt), tt, 128, 0, uT[:, :, tt * 128:(tt + 1) * 128], uT_b)
            if tt + 3 < 8:
                t2 = tt + 3
                k.dma("sp", xt[t2 % 3][0][:, :], x_d[t2 * 128:(t2 + 1) * 128, :], writes=[xt[t2 % 3][1]])
        norm_tile(xh_v[0:8, :], [xh_b], 8, 8, 0, uTh[:, :, :], uTh_b)

        if stop == 2:
            early(uT[:, 3, :].bitcast(F32), uT_b, 512)
            return nc
        if stop == 3:
            early(uTh[:, :, :].rearrange("p a b -> p (a b)").bitcast(F32), uTh_b, 64)
            return nc
        r1.reset()
        fs = [r1.alloc("fs%d" % i, T, F32) for i in range(8)]
        xe_v, xe_b = r1.alloc("xe", T + 8, F32)
        bs = [r1.alloc("bs%d" % i, T, BF16) for i in range(7)]
        t1_v, t1_b = r1.alloc("t1", T, F32)
        t2_v, t2_b = r1.alloc("t2", T, F32)
        wcg = [r2.alloc("wcg%d" % i, 2048, BF16) for i in range(6)]
        tabs = [r2.alloc("tab%d" % i, T, F32) for i in range(4)]
        wcg_i = [0]

        def load_cg(cg):
            v, b = wcg[wcg_i[0] % 6]
            wcg_i[0] += 1
            k.dma("pool", v[:, :], win_d[cg, :, :], writes=[b])
            return v.rearrange("p (c m) -> p c m", m=128), b

        def proj_fm(cg, pi):
            w3, wb_ = load_cg(cg)
            for tb in range(2):
                for kc in range(16):
                    k.op("pe", lambda e, tb=tb, kc=kc: e.matmul(pp[pi][:, tb * 512:(tb + 1) * 512], w3[:, kc, :],
                                                                uT[:, kc, tb * 512:(tb + 1) * 512],
                                                                start=(kc == 0), stop=(kc == 15)),
                         reads=[wb_, uT_b], writes=[pb[2 * pi + tb]], signal=(kc == 15))
            return w3, wb_

        def rope(pi, pj, ctab, stab, dst_v, dst_b, kb):
            kb_v, kb_b = kb
            k.op("act", lambda e: e.activation(out=kb_v[:, :], in_=pp[pi][:, :], func=AF.Copy),
                 reads=pair_b(pi), writes=[kb_b])
            k.op("act", lambda e: e.activation(out=t1_v[:, :], in_=pp[pi][:, :], func=AF.Copy),
                 reads=pair_b(pi), writes=[t1_b])
            for tb in range(2):
                k.op("pe", lambda e, tb=tb: e.matmul(pp[pj][:, tb * 512:(tb + 1) * 512], perm[:, :],
                                                     kb_v[:, tb * 512:(tb + 1) * 512], start=True, stop=True),
                     reads=[perm_b, kb_b], writes=[pb[2 * pj + tb]])
            k.op("dve", lambda e: e.tensor_tensor(out=t1_v[:, :], in0=t1_v[:, :], in1=ctab[0][:, :], op=ALU.mult),
                 reads=[t1_b, ctab[1]], writes=[t1_b])
            k.op("dve", lambda e: e.tensor_tensor(out=t2_v[:, :], in0=pp[pj][:, :], in1=stab[0][:, :], op=ALU.mult),
                 reads=pair_b(pj) + [stab[1]], writes=[t2_b])
            if stop in (44, 45):
                return
            k.op("dve", lambda e: e.tensor_tensor(out=dst_v[:, :], in0=t1_v[:, :], in1=t2_v[:, :], op=ALU.add),
                 reads=[t1_b, t2_b], writes=[dst_b])

        if mode == "B":
            k.op("dve", lambda e: e.memset(hin[:], 0.0), writes=[hin_b])
            for cp in range(8):
                he = lall[:, cp * 16:cp * 16 + 8]
                at = lall[:, cp * 16 + 8:cp * 16 + 16]
                k.op("dve", lambda e, at=at: e.tensor_tensor(out=htmp[:], in0=hin[:], in1=at, op=ALU.mult),
                     reads=[hin_b, lall_b], writes=[htmp_b])
                k.op("dve", lambda e, he=he: e.tensor_tensor(out=htmp[:], in0=htmp[:], in1=he, op=ALU.add),
                     reads=[htmp_b, lall_b], writes=[htmp_b])
                k.op("dve", lambda e: e.tensor_tensor(out=htmp[:], in0=htmp[:], in1=hin[:], op=ALU.subtract),
                     reads=[htmp_b, hin_b], writes=[htmp_b])
                k.op("dve", lambda e, cp=cp: e.scalar_tensor_tensor(out=hin[:], in0=htmp[:],
                                                                     scalar=coef[:, 64 + cp:65 + cp], in1=hin[:],
                                                                     op0=ALU.mult, op1=ALU.add),
                     reads=[htmp_b, hin_b, coef_b], writes=[hin_b])
            racc_v, racc_b = fs[0]
            for cp in range(8):
                rl_v, rl_b = fs[1 + cp % 2]
                k.dma("sp", rl_v[:, :], rall_d[cp, :, :], writes=[rl_b])
                for h in range(8):
                    sl = slice(h * 128, (h + 1) * 128)
                    cf = coef[:, cp * 8 + h:cp * 8 + h + 1]
                    if cp == 0:
                        k.op("dve", lambda e, sl=sl, cf=cf, rl_v=rl_v: e.tensor_scalar(out=racc_v[:, sl], in0=rl_v[:, sl],
                                                                                        scalar1=cf, scalar2=None,
                                                                                        op0=ALU.mult),
                             reads=[rl_b, coef_b], writes=[racc_b])
                    else:
                        k.op("dve", lambda e, sl=sl, cf=cf, rl_v=rl_v: e.scalar_tensor_tensor(
                            out=racc_v[:, sl], in0=rl_v[:, sl], scalar=cf, in1=racc_v[:, sl],
                            op0=ALU.mult, op1=ALU.add), reads=[rl_b, coef_b, racc_b], writes=[racc_b])
            k.op("act", lambda e: e.activation(out=rin[:, :], in_=racc_v[:, :], func=AF.Copy),
                 reads=[racc_b], writes=[rin_b])

        for j in range(8):
            for ti in range(4):
                if mode == "A" and ti < 2:
                    continue
                k.dma("sp", tabs[ti][0][:, :], tab_d[j, ti, :, :], writes=[tabs[ti][1]])
            proj_fm(24 + j, 0)
            KT_v, KT_b = bs[0]
            if stop == 40:
                k.op("act", lambda e: e.activation(out=bs[1][0][:, :], in_=pp[0][:, :], func=AF.Copy),
                     reads=pair_b(0), writes=[bs[1][1]])
                early(bs[1][0][:, :].bitcast(F32), bs[1][1], 512)
                return nc
            if stop == 42:
                k.op("dve", lambda e: e.tensor_tensor(out=t1_v[:, :], in0=pp[0][:, :], in1=tabs[2][0][:, :], op=ALU.mult),
                     reads=pair_b(0) + [tabs[2][1]], writes=[t1_b])
                early(t1_v[:, :], t1_b, 1024)
                return nc
            if stop == 43:
                k.op("act", lambda e: e.activation(out=bs[1][0][:, :], in_=pp[0][:, :], func=AF.Copy),
                     reads=pair_b(0), writes=[bs[1][1]])
                for tb in range(2):
                    k.op("pe", lambda e, tb=tb: e.matmul(pp[2][:, tb * 512:(tb + 1) * 512], perm[:, :],
                                                         bs[1][0][:, tb * 512:(tb + 1) * 512], start=True, stop=True),
                         reads=[perm_b, bs[1][1]], writes=[pb[4 + tb]])
                k.op("act", lambda e: e.activation(out=t1_v[:, :], in_=pp[2][:, :], func=AF.Copy),
                     reads=pair_b(2), writes=[t1_b])
                early(t1_v[:, :], t1_b, 1024)
                return nc
            rope(0, 2, tabs[2], tabs[2] if stop == 45 else tabs[3], KT_v, KT_b, bs[1])
            if stop in (41, 44, 45):
                early(t2_v[:, :], t2_b, 1024)
                return nc
            if stop == 4:
                early(KT_v[:, :].bitcast(F32), KT_b, 512)
                return nc
            pvk = pp[2][:, :].bitcast(BF16)
            for n in range(8):
                k.op("pe", lambda e, n=n: e.transpose(pvk[:, n * 128:(n + 1) * 128], KT_v[:, n * 128:(n + 1) * 128],
                                                      ident[:, :]),
                     reads=[KT_b, ident_b], writes=pair_b(2), signal=(n == 7))
            Kt_v, Kt_b = bs[2]
            k.op("act", lambda e: e.activation(out=Kt_v[:, :], in_=pvk[:, 0:1024], func=AF.Copy),
                 reads=pair_b(2), writes=[Kt_b])
            w3, wb_ = load_cg(32 + j)
            for tt in range(8):
                for kc in range(16):
                    k.op("pe", lambda e, tt=tt, kc=kc, w3=w3: e.matmul(pp[1][:, tt * 128:(tt + 1) * 128],
                                                                        uT[:, kc, tt * 128:(tt + 1) * 128], w3[:, kc, :],
                                                                        start=(kc == 0), stop=(kc == 15)),
                         reads=[wb_, uT_b], writes=[pb[2 + tt // 4]], signal=(kc == 15))
            Vt_v, Vt_b = bs[3]
            k.op("act", lambda e: e.activation(out=Vt_v[:, :], in_=pp[1][:, :], func=AF.Copy),
                 reads=pair_b(1), writes=[Vt_b])
            if stop == 5:
                early(Vt_v[:, :].bitcast(F32), Vt_b, 512)
                return nc
            nlist = range(1, 9) if mode == "B" else [8]
            for n in nlist:
                for m in range(n):
                    k.op("pe", lambda e, n=n, m=m: e.matmul(pp[3][:, (n - 1) * 128:n * 128],
                                                            Kt_v[:, m * 128:(m + 1) * 128],
                                                            Vt_v[:, m * 128:(m + 1) * 128],
                                                            start=(m == 0), stop=(m == n - 1)),
                         reads=[Kt_b, Vt_b], writes=[pb[6 + (n - 1) // 4]], signal=(m == n - 1))
            if mode == "A":
                k.op("act", lambda e, j=j: e.activation(out=fs[0][0][:, j * 128:(j + 1) * 128],
                                                        in_=pp[3][:, 896:1024], func=AF.Copy),
                     reads=[pb[7]], writes=[fs[0][1]])
            else:
                Rl_v, Rl_b = bs[4]
                k.op("act", lambda e: e.activation(out=Rl_v[:, 0:896], in_=pp[3][:, 0:896], func=AF.Copy),
                     reads=pair_b(3), writes=[Rl_b])
                proj_fm(16 + j, 0)
                QT_v, QT_b = bs[5]
                rope(0, 2, tabs[0], tabs[1], QT_v, QT_b, bs[1])
                for n in range(8):
                    sl = slice(n * 128, (n + 1) * 128)
                    k.op("pe", lambda e, sl=sl: e.matmul(pp[2][:, sl], KT_v[:, sl], QT_v[:, sl], start=True, stop=True),
                         reads=[KT_b, QT_b], writes=[pb[4 + n // 4]], signal=(n % 4 == 3))
                SD_v, SD_b = bs[6]
                k.op("dve", lambda e: e.tensor_tensor(out=SD_v[:, :], in0=pp[2][:, :], in1=cmask[:, :], op=ALU.mult),
                     reads=pair_b(2) + [cmask_b], writes=[SD_b])
                for n in range(8):
                    sl = slice(n * 128, (n + 1) * 128)
                    k.op("pe", lambda e, sl=sl: e.matmul(pp[3][:, sl], Vt_v[:, sl], SD_v[:, sl], start=True, stop=False),
                         reads=[Vt_b, SD_b], writes=[pb[6 + n // 4]], signal=False)
                    if n >= 1:
                        k.op("pe", lambda e, sl=sl, n=n: e.matmul(pp[3][:, sl], Rl_v[:, (n - 1) * 128:n * 128],
                                                                  QT_v[:, sl], start=False, stop=False),
                             reads=[Rl_b, QT_b], writes=[pb[6 + n // 4]], signal=False)
                    k.op("pe", lambda e, sl=sl, j=j: e.matmul(pp[3][:, sl], rin[:, j * 128:(j + 1) * 128], QT_v[:, sl],
                                                              start=False, stop=True),
                         reads=[rin_b, QT_b], writes=[pb[6 + n // 4]], signal=(n % 4 == 3))
                osb_v, osb_b = fs[1]
                osq_v, osq_b = fs[2]
                k.op("act", lambda e: e.activation(out=osb_v[:, :], in_=pp[3][:, :], func=AF.Copy),
                     reads=pair_b(3), writes=[osb_b])
                k.op("act", lambda e: e.activation(out=osq_v[:, :], in_=pp[3][:, :], func=AF.Square),
                     reads=pair_b(3), writes=[osq_b])
                for tb in range(2):
                    sl = slice(tb * 512, (tb + 1) * 512)
                    k.op("pe", lambda e, sl=sl: e.matmul(pp[2][:, sl], onesf[:, :], osb_v[:, sl], start=True, stop=True),
                         reads=[onesf_b, osb_b], writes=[pb[4 + tb]])
                    k.op("pe", lambda e, sl=sl: e.matmul(pp[0][:, sl], onesf[:, :], osq_v[:, sl], start=True, stop=True),
                         reads=[onesf_b, osq_b], writes=[pb[0 + tb]])
                mean_v, mean_b = fs[3]
                var_v, var_b = fs[4]
                k.op("act", lambda e: e.activation(out=mean_v[:, :], in_=pp[2][:, :], func=AF.Copy),
                     reads=pair_b(2), writes=[mean_b])
                k.op("dve", lambda e: e.tensor_tensor(out=var_v[:, :], in0=mean_v[:, :], in1=mean_v[:, :], op=ALU.mult),
                     reads=[mean_b], writes=[var_b])
                k.op("dve", lambda e: e.tensor_tensor(out=var_v[:, :], in0=pp[0][:, :], in1=var_v[:, :], op=ALU.subtract),
                     reads=pair_b(0) + [var_b], writes=[var_b])
                k.op("act", lambda e: e.activation(out=var_v[:, :], in_=var_v[:, :], func=AF.Sqrt, bias=epst[:, :],
                                                   scale=1.0), reads=[var_b, epst_b], writes=[var_b])
                k.op("dve", lambda e: e.reciprocal(out=var_v[:, :], in_=var_v[:, :]), reads=[var_b], writes=[var_b])
                k.op("dve", lambda e: e.tensor_tensor(out=osb_v[:, :], in0=osb_v[:, :], in1=mean_v[:, :], op=ALU.subtract),
                     reads=[osb_b, mean_b], writes=[osb_b])
                k.op("dve", lambda e: e.tensor_tensor(out=osb_v[:, :], in0=osb_v[:, :], in1=var_v[:, :], op=ALU.mult),
                     reads=[osb_b, var_b], writes=[osb_b])
                proj_fm(40 + j, 1)
                sg_v, sg_b = fs[5]
                k.op("act", lambda e: e.activation(out=sg_v[:, :], in_=pp[1][:, :], func=AF.Silu),
                     reads=pair_b(1), writes=[sg_b])
                k.op("dve", lambda e, j=j: e.scalar_tensor_tensor(out=yT[:, 8 + j, :], in0=osb_v[:, :],
                                                                   scalar=smallp[:, 72 + j:73 + j], in1=sg_v[:, :],
                                                                   op0=ALU.mult, op1=ALU.mult),
                     reads=[osb_b, sg_b, smallp_b], writes=[yT_b[8 + j]])
            if stop == 6:
                early(fs[0][0][:, :], fs[0][1], 1024)
                return nc
            w3, wb_ = proj_fm(j, 0)
            for kc in range(16):
                k.op("pe", lambda e, kc=kc, w3=w3: e.matmul(pp[2][:, 0:8], w3[:, kc, :], uTh[:, kc, :],
                                                            start=(kc == 0), stop=(kc == 15)),
                     reads=[wb_, uTh_b], writes=[pb[4]], signal=(kc == 15))
            k.op("act", lambda e: e.activation(out=xe_v[:, 8:T + 8], in_=pp[0][:, :], func=AF.Copy),
                 reads=pair_b(0), writes=[xe_b])
            k.op("act", lambda e: e.activation(out=xe_v[:, 0:8], in_=pp[2][:, 0:8], func=AF.Copy),
                 reads=[pb[4]], writes=[xe_b])
            cx_v, cx_b = fs[6]
            k.op("dve", lambda e, j=j: e.tensor_scalar(out=cx_v[:, :], in0=xe_v[:, 8:T + 8], scalar1=spj(j, 3),
                                                       scalar2=spj(j, 4), op0=ALU.mult, op1=ALU.add),
                 reads=[xe_b, smallp_b], writes=[cx_b])
            for tap in range(3):
                k.op("dve", lambda e, j=j, tap=tap: e.scalar_tensor_tensor(out=cx_v[:, :], in0=xe_v[:, 5 + tap:5 + tap + T],
                                                                           scalar=spj(j, tap), in1=cx_v[:, :],
                                                                           op0=ALU.mult, op1=ALU.add),
                     reads=[xe_b, smallp_b, cx_b], writes=[cx_b])
            cxb_v, cxb_b = bs[1]
            k.op("act", lambda e: e.activation(out=cxb_v[:, :], in_=cx_v[:, :], func=AF.Copy),
                 reads=[cx_b], writes=[cxb_b])
            for tb in range(2):
                sl = slice(tb * 512, (tb + 1) * 512)
                k.op("pe", lambda e, sl=sl, j=j: e.matmul(pp[2][:, sl], wa[:, j * 128:(j + 1) * 128], cxb_v[:, sl],
                                                          start=True, stop=True),
                     reads=[wa_b, cxb_b], writes=[pb[4 + tb]])
                k.op("pe", lambda e, sl=sl, j=j: e.matmul(pp[1][:, sl], wx[:, j * 128:(j + 1) * 128], cxb_v[:, sl],
                                                          start=True, stop=True),
                     reads=[wx_b, cxb_b], writes=[pb[2 + tb]])
            rr_v, rr_b = fs[7]
            ii_v, ii_b = fs[5]
            aa_v, aa_b = fs[1]
            sq_v, sq_b = fs[2]
            k.op("act", lambda e, j=j: e.activation(out=rr_v[:, :], in_=pp[2][:, :], func=AF.Sigmoid, bias=spj(j, 5),
                                                    scale=1.0, accum_out=sumr[:, 0:1]),
                 reads=pair_b(2) + [smallp_b], writes=[rr_b, sumr_b])
            k.op("act", lambda e, j=j: e.activation(out=ii_v[:, :], in_=pp[1][:, :], func=AF.Sigmoid, bias=spj(j, 6),
                                                    scale=1.0),
                 reads=pair_b(1) + [smallp_b], writes=[ii_b])
            k.op("act", lambda e, j=j: e.activation(out=aa_v[:, :], in_=rr_v[:, :], func=AF.Exp, scale=c8[:, j:j + 1]),
                 reads=[rr_b, c8_b], writes=[aa_b])
            k.op("act", lambda e, j=j: e.activation(out=sq_v[:, :], in_=rr_v[:, :], func=AF.Exp,
                                                    scale=c8[:, 8 + j:9 + j]),
                 reads=[rr_b, c8_b], writes=[sq_b])
            k.op("act", lambda e: e.activation(out=sq_v[:, :], in_=sq_v[:, :], func=AF.Sqrt, bias=onet[:, :], scale=-1.0),
                 reads=[sq_b, onet_b], writes=[sq_b])
            k.op("dve", lambda e: e.tensor_tensor(out=ii_v[:, :], in0=ii_v[:, :], in1=cx_v[:, :], op=ALU.mult),
                 reads=[ii_b, cx_b], writes=[ii_b])
            k.op("dve", lambda e: e.tensor_tensor(out=ii_v[:, :], in0=ii_v[:, :], in1=sq_v[:, :], op=ALU.mult),
                 reads=[ii_b, sq_b], writes=[ii_b])
            hl_v, hl_b = fs[3]
            if stop == 7:
                early(ii_v[:, :], ii_b, 1024)
                return nc
            if mode == "A":
                k.op("dve", lambda e: e.tensor_tensor_scan(out=hl_v[:, :], data0=aa_v[:, :], data1=ii_v[:, :], initial=0.0,
                                                           op0=ALU.mult, op1=ALU.add),
                     reads=[aa_b, ii_b], writes=[hl_b])
                k.op("dve", lambda e, j=j: e.tensor_copy(out=stl[:, j:j + 1], in_=hl_v[:, T - 1:T]),
                     reads=[hl_b], writes=[stl_b])
                k.op("act", lambda e, j=j: e.activation(out=stl[:, 8 + j:9 + j], in_=sumr[:, 0:1], func=AF.Exp,
                                                        scale=c8[:, j:j + 1]),
                     reads=[sumr_b, c8_b], writes=[stl_b])
            else:
                k.op("dve", lambda e, j=j: e.tensor_tensor_scan(out=hl_v[:, :], data0=aa_v[:, :], data1=ii_v[:, :],
                                                                initial=hin[:, j:j + 1], op0=ALU.mult, op1=ALU.add),
                     reads=[aa_b, ii_b, hin_b], writes=[hl_b])
                proj_fm(8 + j, 0)
                gl_v, gl_b = fs[4]
                if GELU_TANH_NATIVE:
                    k.op("act", lambda e: e.activation(out=gl_v[:, :], in_=pp[0][:, :], func=AF.Gelu_apprx_tanh),
                         reads=pair_b(0), writes=[gl_b])
                else:
                    x2_v, x2_b = fs[6]
                    xg_v, xg_b = fs[0]
                    k.op("act", lambda e: e.activation(out=xg_v[:, :], in_=pp[0][:, :], func=AF.Copy),
                         reads=pair_b(0), writes=[xg_b])
                    k.op("act", lambda e: e.activation(out=x2_v[:, :], in_=pp[0][:, :], func=AF.Square),
                         reads=pair_b(0), writes=[x2_b])
                    k.op("dve", lambda e: e.tensor_scalar(out=x2_v[:, :], in0=x2_v[:, :], scalar1=0.044715, scalar2=1.0,
                                                          op0=ALU.mult, op1=ALU.add), reads=[x2_b], writes=[x2_b])
                    k.op("dve", lambda e: e.tensor_tensor(out=x2_v[:, :], in0=xg_v[:, :], in1=x2_v[:, :], op=ALU.mult),
                         reads=[xg_b, x2_b], writes=[x2_b])
                    k.op("act", lambda e: e.activation(out=x2_v[:, :], in_=x2_v[:, :], func=AF.Sigmoid,
                                                       scale=1.5957691216057308), reads=[x2_b], writes=[x2_b])
                    k.op("dve", lambda e: e.tensor_tensor(out=gl_v[:, :], in0=xg_v[:, :], in1=x2_v[:, :], op=ALU.mult),
                         reads=[xg_b, x2_b], writes=[gl_b])
                k.op("dve", lambda e, j=j: e.tensor_tensor(out=yT[:, j, :], in0=hl_v[:, :], in1=gl_v[:, :], op=ALU.mult),
                     reads=[hl_b, gl_b], writes=[yT_b[j]])

        if mode == "A":
            k.dma("sp", st_d[:, 0:1024], fs[0][0][:, :], reads=[fs[0][1]])
            k.dma("sp", st_d[:, 1024:1040], stl[:, :], reads=[stl_b])
            k.finish()
            k.emit(block)
            return nc

        r1.reset()
        r2.reset()
        h1 = [r1.alloc("h1_%d" % i, DM, F32) for i in range(8)]
        for tt in range(8):
            k.dma("sp", h1[tt][0][:, :], x_d[tt * 128:(tt + 1) * 128, :], writes=[h1[tt][1]])
        wo = [r2.alloc("wo%d" % i, 8192, BF16) for i in range(2)]
        bank_i = [0]

        def next_bank():
            b = bank_i[0] % 8
            bank_i[0] += 1
            return b

        for db in range(4):
            wo_v, wo_b = wo[db % 2]
            for q4 in range(4):
                k.dma("pool", wo_v[:, q4 * 2048:(q4 + 1) * 2048], wo_d[db, :, q4 * 2048:(q4 + 1) * 2048], writes=[wo_b])
            wo3 = wo_v.rearrange("p (c n) -> p c n", n=512)
            for tt in range(8):
                bi = next_bank()
                pv = pp[bi // 2][:, (bi % 2) * 512:(bi % 2 + 1) * 512]
                for ec in range(16):
                    k.op("pe", lambda e, ec=ec, tt=tt, pv=pv, wo3=wo3: e.matmul(pv, yT[:, ec, tt * 128:(tt + 1) * 128],
                                                                                wo3[:, ec, :], start=(ec == 0),
                                                                                stop=(ec == 15)),
                         reads=[wo_b, yT_b[ec]], writes=[pb[bi]], signal=(ec == 15))
                hv = h1[tt][0][:, db * 512:(db + 1) * 512]
                k.op("dve", lambda e, pv=pv, hv=hv: e.tensor_tensor(out=hv, in0=pv, in1=hv, op=ALU.add),
                     reads=[pb[bi], h1[tt][1]], writes=[h1[tt][1]])

        r2.reset()
        junk0, junk0_b = r2.alloc("junk", DM, BF16)
        ut0 = [r2.alloc("ut%d" % i, DM, BF16) for i in range(2)]
        for tt in range(8):
            norm_tile(h1[tt][0][:, :], [h1[tt][1]], tt, 128, 16, uT[:, :, tt * 128:(tt + 1) * 128], uT_b)

        r2.reset()
        h2T_v, h2T_b = r2.alloc("h2T", 22 * T, BF16)
        h2T3 = h2T_v.rearrange("p (f t) -> p f t", t=T)
        yT_all = Buf("yTring")
        yT_all.inherit(yT_b)
        NW = 6
        wgu = [(yT[:, 2 * i:2 * i + 2, :].rearrange("p a t -> p (a t)"), Buf("wgu%d" % i)) for i in range(NW)]
        for _, b in wgu:
            b.inherit(yT_b)
        sgr = [yT[:, 12 + i, :] for i in range(2)]
        sgr_b = [Buf("sgr%d" % i) for i in range(2)]
        for b in sgr_b:
            b.inherit(yT_b)
        wgu_i = [0]

        def load_gu(src_d, f):
            v, b = wgu[wgu_i[0] % NW]
            wgu_i[0] += 1
            k.dma("pool", v, src_d[f, :, :], writes=[b])
            return v.rearrange("p (c m) -> p c m", m=128), b

        wd_i = [0]
        for half in range(2):
            for fl in range(22):
                f = half * 22 + fl
                g3, g_b = load_gu(wg_d, f)
                u3, u_b = load_gu(wu_d, f)
                pg = (fl % 2) * 2
                for (w3, w_b, pi) in ((g3, g_b, pg), (u3, u_b, pg + 1)):
                    for tb in range(2):
                        for kc in range(16):
                            k.op("pe", lambda e, tb=tb, kc=kc, w3=w3, pi=pi: e.matmul(
                                pp[pi][:, tb * 512:(tb + 1) * 512], w3[:, kc, :], uT[:, kc, tb * 512:(tb + 1) * 512],
                                start=(kc == 0), stop=(kc == 15)),
                                 reads=[w_b, uT_b], writes=[pb[2 * pi + tb]], signal=(kc == 15))
                sgv, sgb = sgr[fl % 2], sgr_b[fl % 2]
                k.op("act", lambda e, sgv=sgv, pg=pg: e.activation(out=sgv[:, :], in_=pp[pg][:, :], func=AF.Silu),
                     reads=pair_b(pg), writes=[sgb])
                k.op("dve", lambda e, sgv=sgv, pg=pg, fl=fl: e.tensor_tensor(out=h2T3[:, fl, :], in0=pp[pg + 1][:, :],
                                                                              in1=sgv[:, :], op=ALU.mult),
                     reads=pair_b(pg + 1) + [sgb], writes=[h2T_b])
            for db in range(8):
                wd_v, wd_b = wdr[wd_i[0] % 2], wdr_b[wd_i[0] % 2]
                wd_i[0] += 1
                base = half * 22 * 256
                for (f0, f1) in ((0, 8), (8, 16), (16, 22)):
                    k.dma("pool", wd_v[:, f0:f1, :].rearrange("p f n -> p (f n)"),
                          wd_d[db, :, base + f0 * 256:base + f1 * 256], writes=[wd_b])
                for tt in range(8):
                    bi = next_bank()
                    pv = pp[bi // 2][:, (bi % 2) * 512:(bi % 2) * 512 + 256]
                    for fl in range(22):
                        k.op("pe", lambda e, fl=fl, tt=tt, pv=pv, wd_v=wd_v: e.matmul(
                            pv, h2T3[:, fl, tt * 128:(tt + 1) * 128], wd_v[:, fl, :], start=(fl == 0), stop=(fl == 21)),
                             reads=[wd_b, h2T_b], writes=[pb[bi]], signal=(fl == 21))
                    hv = h1[tt][0][:, db * 256:(db + 1) * 256]
                    k.op("dve", lambda e, pv=pv, hv=hv: e.tensor_tensor(out=hv, in0=pv, in1=hv, op=ALU.add),
                         reads=[pb[bi], h1[tt][1]], writes=[h1[tt][1]])

        r2.reset()
        fnw, fnw_b = r2.alloc("fnw", DM, F32)
        k.dma("sp", fnw[:, :], fnw_d[:, :], writes=[fnw_b])
        for tt in range(8):
            hv, hb = h1[tt]
            s_t, s_b = ss[tt % 2], ss_b[tt % 2]
            jk5 = uT[:, 0:2, :].rearrange("p a t -> p (a t)")
            k.op("act", lambda e, hv=hv, s_t=s_t, jk5=jk5: e.activation(out=jk5, in_=hv[:, :], func=AF.Square,
                                                                         accum_out=s_t[:, 0:1]),
                 reads=[hb], writes=[uT_b, s_b])
            k.op("act", lambda e, s_t=s_t: e.activation(out=s_t[:, 1:2], in_=s_t[:, 0:1], func=AF.Sqrt, bias=epst[:, :],
                                                        scale=1.0 / DM), reads=[s_b, epst_b], writes=[s_b])
            k.op("dve", lambda e, s_t=s_t: e.reciprocal(out=s_t[:, 0:1], in_=s_t[:, 1:2]), reads=[s_b], writes=[s_b])
            k.op("dve", lambda e, hv=hv, s_t=s_t: e.scalar_tensor_tensor(out=hv[:, :], in0=hv[:, :], scalar=s_t[:, 0:1],
                                                                          in1=fnw[:, :], op0=ALU.mult, op1=ALU.mult),
                 reads=[hb, s_b, fnw_b], writes=[hb])
            k.dma("sp", y_d[tt * 128:(tt + 1) * 128, :], hv[:, :], reads=[hb])
        k.finish()
        k.emit(block)
    return nc


def _bf16(a):
    return np.asarray(a, dtype=np.float32).astype(ml_dtypes.bfloat16)


def _cg_layout(w, ncol_chunks):
    C = w.shape[1]
    a = w.reshape(16, 128, C // 128, 128)
    return np.ascontiguousarray(a.transpose(2, 1, 0, 3)).reshape(C // 128, 128, 2048)


_NC_CACHE = {}


def _get_nc(mode):
    if mode not in _NC_CACHE:
        _NC_CACHE[mode] = build_nc(mode)
    return _NC_CACHE[mode]


def kernel(x, ln1_w, w_in, conv_w, conv_b, gate_a_w, gate_a_b, gate_x_w, gate_x_b, lru_lambda, ret_gn_w, w_out,
           ln2_w, w_ffn_gate, w_ffn_up, w_ffn_down, final_norm_w):
    f32 = np.float32
    x = np.asarray(x, f32)[0]
    win = _cg_layout(np.asarray(w_in, f32)[0], 48)
    wg = _cg_layout(np.asarray(w_ffn_gate, f32)[0], NF)
    wu = _cg_layout(np.asarray(w_ffn_up, f32)[0], NF)
    wo_full = np.asarray(w_out, f32)[0]
    wo = np.ascontiguousarray(wo_full.reshape(16, 128, 4, 512).transpose(2, 1, 0, 3)).reshape(4, 128, 8192)
    wd_full = np.asarray(w_ffn_down, f32)[0]
    wd = np.ascontiguousarray(wd_full.reshape(NF, 128, 8, 256).transpose(2, 1, 0, 3)).reshape(8, 128, NF * 256)
    war = np.ascontiguousarray(np.asarray(gate_a_w, f32)[0].transpose(1, 0, 2)).reshape(128, 1024)
    wxr = np.ascontiguousarray(np.asarray(gate_x_w, f32)[0].transpose(1, 0, 2)).reshape(128, 1024)
    lnw = np.zeros((128, 32), f32)
    lnw[:, 0:16] = np.asarray(ln1_w, f32)[0].reshape(16, 128).T
    lnw[:, 16:32] = np.asarray(ln2_w, f32)[0].reshape(16, 128).T
    smallp = np.zeros((128, 80), f32)
    sp3 = smallp[:, 0:72].reshape(128, 8, 9)
    cw = np.asarray(conv_w, f32)[0]
    for tap in range(4):
        sp3[:, :, tap] = cw[tap].reshape(8, 128).T
    sp3[:, :, 4] = np.asarray(conv_b, f32)[0].reshape(8, 128).T
    sp3[:, :, 5] = np.asarray(gate_a_b, f32)[0].reshape(8, 128).T
    sp3[:, :, 6] = np.asarray(gate_x_b, f32)[0].reshape(8, 128).T
    sp3[:, :, 7] = np.asarray(lru_lambda, f32)[0].reshape(8, 128).T
    smallp[:, 72:80] = np.asarray(ret_gn_w, f32)[0].reshape(8, 128).T
    fnw = np.ascontiguousarray(np.broadcast_to(np.asarray(final_norm_w, f32)[None, :], (128, DM)))
    ident = _bf16(np.eye(128))
    pm = np.zeros((128, 128), f32)
    for m in range(128):
        pm[(m + 64) % 128, m] = 1.0
    perm = _bf16(pm)
    kq = np.arange(128)
    cm = (kq[None, :] >= kq[:, None]).astype(f32)
    cmask = _bf16(np.tile(cm, (1, 8)))
    onesf = np.full((128, 128), 1.0 / 128.0, f32)
    gam = 1.0 - np.exp2(-5.0 - np.arange(8, dtype=np.float64))
    inv_freq = 10000.0 ** (-np.arange(0, 128, 2, dtype=np.float64) / 128.0)
    tl = np.arange(T, dtype=np.float64)
    tabs = []
    for c in range(NCORES):
        pos = c * T + tl
        ang = inv_freq[:, None] * pos[None, :]
        cosd = np.concatenate([np.cos(ang), np.cos(ang)], 0)
        sind = np.concatenate([-np.sin(ang), np.sin(ang)], 0)
        tb = np.zeros((8, 4, 128, T), f32)
        for h in range(8):
            gq = gam[h] ** (tl + 1.0)
            gk = gam[h] ** (-(tl + 1.0)) * (128.0 ** -0.5)
            tb[h, 0] = cosd * gq[None, :]
            tb[h, 1] = sind * gq[None, :]
            tb[h, 2] = cosd * gk[None, :]
            tb[h, 3] = sind * gk[None, :]
        tabs.append(tb)

    common = {"win": win, "war": war, "wxr": wxr, "lnw": lnw, "smallp": smallp, "ident": ident, "perm": perm}
    in_a = []
    for c in range(NCORES):
        xh = np.zeros((8, DM), f32)
        if c > 0:
            xh[5:8] = x[c * T - 3:c * T]
        d = dict(common)
        d.update({"x": np.ascontiguousarray(x[c * T:(c + 1) * T]), "xh": xh, "tab": tabs[c]})
        in_a.append(d)
    ra = run_bass_kernel_spmd(_get_nc("A"), in_a, core_ids=list(range(NCORES)))
    st = [np.asarray(ra.results[c]["st"], f32) for c in range(NCORES)]
    rall = np.ascontiguousarray(np.stack([s[:, 0:1024] for s in st], 0))
    lall = np.ascontiguousarray(np.concatenate([s[:, 1024:1040] for s in st], 1))
    in_b = []
    for c in range(NCORES):
        coef = np.zeros((128, 72), f32)
        for cp in range(8):
            if cp < c:
                coef[:, cp * 8:cp * 8 + 8] = (gam ** (1024.0 * (c - cp))).astype(f32)[None, :]
                coef[:, 64 + cp] = 1.0
        d = dict(in_a[c])
        d.update({"wo": wo, "wg": wg, "wu": wu, "wd": wd, "fnw": fnw, "cmask": cmask, "onesf": onesf,
                  "rall": rall, "lall": lall, "coef": coef})
        in_b.append(d)
    rb = run_bass_kernel_spmd(_get_nc("B"), in_b, core_ids=list(range(NCORES)))
    out = np.concatenate([np.asarray(rb.results[c]["y"], f32) for c in range(NCORES)], 0)
    return out.reshape(1, NCORES * T, DM)
```
